# Optimizing a Trainium2 kernel written in Bass

```python
import math
import jax, jax.numpy as jnp
from jax import lax
import numpy as np


D_MODEL = 2048
BATCH = 4
SEQ = 2048
DEPTH = 2
DEC_BATCH = 8
DEC_SEQ = 4
PAST_LEN = 16384
PAGE_SIZE = 128

MIX_WIDTH = D_MODEL
D_SSM = MIX_WIDTH // 2
SSM_HEAD_DIM = 64
SSM_HEADS = D_SSM // SSM_HEAD_DIM
SSM_GROUPS = 2
SSM_STATE = 128
CONV_W = 4
CONV_DIM = D_SSM + 2 * SSM_GROUPS * SSM_STATE
SSD_CHUNK = 128
D_ATTN = MIX_WIDTH - D_SSM
DA_HEAD_DIM = 64
DA_V_DIM = 2 * DA_HEAD_DIM
DA_HEADS = D_ATTN // DA_V_DIM
Q_BLOCK = 128
REL_BUCKETS = 32
REL_MAX_DIST = 128
PROJ_DIM = D_SSM + CONV_DIM + SSM_HEADS + 3 * D_ATTN
PEER_HEADS = 8
PEER_DK = 256
N_KEYS = 128
N_EXPERTS = N_KEYS * N_KEYS
PEER_TOPK = 16
PEER_BLOCK = 128
EPS = 1e-6

kernel_name = "hymba_ssd_diffattn_peer_step"


def rms_normalize(x):
    xf = x.astype(jnp.float32)
    return (xf * lax.rsqrt(jnp.mean(xf * xf, axis=-1, keepdims=True) + EPS)).astype(x.dtype)


def rmsnorm(x, g):
    return rms_normalize(x) * g.astype(x.dtype)


def rel_bucket(rel):
    n = jnp.maximum(rel, 0)
    max_exact = REL_BUCKETS // 2
    nf = jnp.maximum(n, 1).astype(jnp.float32)
    large = max_exact + (jnp.log(nf / max_exact) / math.log(REL_MAX_DIST / max_exact)
                         * (REL_BUCKETS - max_exact)).astype(jnp.int32)
    large = jnp.minimum(large, REL_BUCKETS - 1)
    return jnp.where(n < max_exact, n, large)


def diff_attend(q, k, v, q_pos, k_pos, lam, rel_bias):
    logits = jnp.einsum('bqhmd,bkhmd->bhmqk', q, k).astype(jnp.float32) * (DA_HEAD_DIM ** -0.5)
    bias = jnp.transpose(rel_bias[rel_bucket(q_pos[:, None] - k_pos[None, :])], (2, 0, 1)).astype(jnp.float32)
    causal = k_pos[None, :] <= q_pos[:, None]
    logits = jnp.where(causal, logits + bias[None, :, None], -jnp.inf)
    p = jax.nn.softmax(logits, axis=-1)
    a = p[:, :, 0] - lam * p[:, :, 1]
    return jnp.einsum('bhqk,bkhe->bqhe', a, v.astype(jnp.float32)).astype(v.dtype)


def ssd_scan(x, dt, a, bm, cm, h0, chunk):
    b, L, h, p = x.shape
    nc = L // chunk
    hpg = h // SSM_GROUPS
    xf = x.astype(jnp.float32).reshape(b, nc, chunk, h, p)
    bh = jnp.repeat(bm.astype(jnp.float32), hpg, axis=2).reshape(b, nc, chunk, h, SSM_STATE)
    ch = jnp.repeat(cm.astype(jnp.float32), hpg, axis=2).reshape(b, nc, chunk, h, SSM_STATE)
    dtc = dt.reshape(b, nc, chunk, h)
    a_cum = jnp.cumsum(dtc * a, axis=2)
    seg = a_cum[:, :, :, None, :] - a_cum[:, :, None, :, :]
    tril = (jnp.arange(chunk)[:, None] >= jnp.arange(chunk)[None, :])[:, :, None]
    lmat = jnp.exp(jnp.where(tril, seg, -jnp.inf))
    xdt = xf * dtc[..., None]
    scores = jnp.einsum('bclhn,bcshn->bclsh', ch, bh) * lmat
    y_diag = jnp.einsum('bclsh,bcshp->bclhp', scores, xdt)
    decay_to_end = jnp.exp(a_cum[:, :, -1:, :] - a_cum)
    chunk_states = jnp.einsum('bclhn,bclhp->bchpn', bh * decay_to_end[..., None], xdt)
    chunk_decay = jnp.exp(a_cum[:, :, -1, :])

    def step(hc, inp):
        dec, st = inp
        return dec[:, :, None, None] * hc + st, hc

    h_fin, h_prev = lax.scan(step, h0.astype(jnp.float32),
                             (jnp.moveaxis(chunk_decay, 1, 0), jnp.moveaxis(chunk_states, 1, 0)))
    h_prev = jnp.moveaxis(h_prev, 0, 1)
    y_off = jnp.einsum('bclhn,bchpn->bclhp', ch * jnp.exp(a_cum)[..., None], h_prev)
    y = (y_diag + y_off).reshape(b, L, h, p)
    return y.astype(x.dtype), h_fin.astype(x.dtype)


def mixer(n, p, lam_init, rel_bias, conv_buf, ssm_h, past_k, past_v, q_offset, chunk):
    bsz, L, _ = n.shape
    proj = n @ p['w_in']
    o1 = D_SSM
    o2 = o1 + CONV_DIM
    o3 = o2 + SSM_HEADS
    o4 = o3 + D_ATTN
    o5 = o4 + D_ATTN
    z, xbc, dt_raw = proj[..., :o1], proj[..., o1:o2], proj[..., o2:o3]
    q, k, v = proj[..., o3:o4], proj[..., o4:o5], proj[..., o5:]

    xpad = jnp.concatenate([conv_buf.astype(xbc.dtype), xbc], axis=1)
    new_buf = xpad[:, -(CONV_W - 1):]
    xc = lax.conv_general_dilated(xpad, p['conv_w'][:, None, :], (1,), 'VALID',
                                  dimension_numbers=('NWC', 'WIO', 'NWC'),
                                  feature_group_count=CONV_DIM)
    xc = jax.nn.silu(xc + p['conv_b'])
    gn = SSM_GROUPS * SSM_STATE
    xs = xc[..., :D_SSM].reshape(bsz, L, SSM_HEADS, SSM_HEAD_DIM)
    bm = xc[..., D_SSM:D_SSM + gn].reshape(bsz, L, SSM_GROUPS, SSM_STATE)
    cm = xc[..., D_SSM + gn:].reshape(bsz, L, SSM_GROUPS, SSM_STATE)
    dt = jax.nn.softplus(dt_raw.astype(jnp.float32) + p['dt_bias'].astype(jnp.float32))
    a = -jnp.exp(p['a_log'].astype(jnp.float32))
    y, h_fin = ssd_scan(xs, dt, a, bm, cm, ssm_h, chunk)
    y = (y + p['d_skip'][:, None] * xs).reshape(bsz, L, D_SSM) * jax.nn.silu(z)
    y = rms_normalize(y.reshape(bsz, L, SSM_GROUPS, D_SSM // SSM_GROUPS)).reshape(bsz, L, D_SSM) * p['ssm_norm_g']

    lq = p['lam_qk'].astype(jnp.float32)
    lam = jnp.exp(jnp.sum(lq[0] * lq[1])) - jnp.exp(jnp.sum(lq[2] * lq[3])) + lam_init
    q = q.reshape(bsz, L, DA_HEADS, 2, DA_HEAD_DIM)
    k = k.reshape(bsz, L, DA_HEADS, 2, DA_HEAD_DIM)
    v = v.reshape(bsz, L, DA_HEADS, DA_V_DIM)
    if past_k is None:
        k_pos = jnp.arange(L)
        nb = L // Q_BLOCK
        qb = jnp.moveaxis(q.reshape(bsz, nb, Q_BLOCK, DA_HEADS, 2, DA_HEAD_DIM), 1, 0)

        def blk(args):
            qi, i = args
            return diff_attend(qi, k, v, i * Q_BLOCK + jnp.arange(Q_BLOCK), k_pos, lam, rel_bias)

        o = lax.map(blk, (qb, jnp.arange(nb)))
        o = jnp.moveaxis(o, 0, 1).reshape(bsz, L, DA_HEADS, DA_V_DIM)
    else:
        keys = jnp.concatenate([past_k.reshape(bsz, -1, DA_HEADS, 2, DA_HEAD_DIM).astype(k.dtype), k], axis=1)
        vals = jnp.concatenate([past_v.reshape(bsz, -1, DA_HEADS, DA_V_DIM).astype(v.dtype), v], axis=1)
        k_pos = jnp.arange(keys.shape[1])
        o = diff_attend(q, keys, vals, q_offset + jnp.arange(L), k_pos, lam, rel_bias)
    o = rmsnorm(o, p['subln_g']) * (1.0 - lam_init)

    out = jnp.concatenate([y, o.reshape(bsz, L, D_ATTN)], axis=-1) @ p['w_out']
    return out, k.reshape(bsz, L, DA_HEADS, 2 * DA_HEAD_DIM), v, h_fin, new_buf


def peer_tokens(xt, wq, sk1, sk2, u_tab, v_tab):
    t = xt.shape[0]
    q = (xt @ wq).astype(jnp.float32).reshape(t, PEER_HEADS, PEER_DK)
    half = PEER_DK // 2
    s1 = jnp.einsum('thd,hkd->thk', q[..., :half], sk1.astype(jnp.float32))
    s2 = jnp.einsum('thd,hkd->thk', q[..., half:], sk2.astype(jnp.float32))
    v1, i1 = lax.top_k(s1, PEER_TOPK)
    v2, i2 = lax.top_k(s2, PEER_TOPK)
    cand = (v1[..., :, None] + v2[..., None, :]).reshape(t, PEER_HEADS, PEER_TOPK * PEER_TOPK)
    sv, si = lax.top_k(cand, PEER_TOPK)
    e = (jnp.take_along_axis(i1, si // PEER_TOPK, axis=-1) * N_KEYS
         + jnp.take_along_axis(i2, si % PEER_TOPK, axis=-1))
    g = jax.nn.softmax(sv, axis=-1)
    act = jax.nn.gelu(jnp.einsum('thkd,td->thk', u_tab[e], xt).astype(jnp.float32), approximate=False)
    out = jnp.einsum('thk,thkd->td', g * act, v_tab[e].astype(jnp.float32))
    return out.astype(xt.dtype)


def peer(x, wq, sk1, sk2, u_tab, v_tab):
    bsz, L, d = x.shape
    xt = x.reshape(-1, d)
    t = xt.shape[0]
    if t % PEER_BLOCK == 0 and t > PEER_BLOCK:
        out = lax.map(lambda blk: peer_tokens(blk, wq, sk1, sk2, u_tab, v_tab),
                      xt.reshape(t // PEER_BLOCK, PEER_BLOCK, d)).reshape(t, d)
    else:
        out = peer_tokens(xt, wq, sk1, sk2, u_tab, v_tab)
    return out.reshape(bsz, L, d)


def setup_inputs(seed: int = 0) -> dict:
    key = jax.random.key(seed)
    ks = jax.random.split(key, 32)
    nrm = jax.random.normal
    n_pages = PAST_LEN // PAGE_SIZE
    n_used = DEC_BATCH * n_pages
    n_pool = n_used + max(1, n_used // 4)
    page_table = jax.random.permutation(ks[0], n_pool)[:n_used].reshape(DEC_BATCH, n_pages).astype(jnp.int32)
    dt0 = jnp.exp(jax.random.uniform(ks[1], (DEPTH, SSM_HEADS), minval=math.log(1e-3), maxval=math.log(1e-1)))
    return {
        "x_prompt": nrm(ks[2], (BATCH, SEQ, D_MODEL), jnp.float32),
        "x_sample": nrm(ks[3], (DEC_BATCH, DEC_SEQ, D_MODEL), jnp.float32),
        "cache_k": nrm(ks[4], (DEPTH, n_pool, PAGE_SIZE, DA_HEADS, 2 * DA_HEAD_DIM), jnp.float32),
        "cache_v": nrm(ks[5], (DEPTH, n_pool, PAGE_SIZE, DA_HEADS, DA_V_DIM), jnp.float32),
        "state_ssm": 0.5 * nrm(ks[6], (DEPTH, DEC_BATCH, SSM_HEADS, SSM_HEAD_DIM, SSM_STATE), jnp.float32),
        "state_conv": nrm(ks[7], (DEPTH, DEC_BATCH, CONV_W - 1, CONV_DIM), jnp.float32),
        "page_table": page_table,
        "rel_bias": 0.5 * nrm(ks[8], (REL_BUCKETS, DA_HEADS), jnp.float32),
        "norm_attn_g": 1.0 + 0.02 * nrm(ks[9], (DEPTH, D_MODEL), jnp.float32),
        "w_in": nrm(ks[10], (DEPTH, D_MODEL, PROJ_DIM), jnp.float32) * D_MODEL ** -0.5,
        "conv_w": nrm(ks[11], (DEPTH, CONV_W, CONV_DIM), jnp.float32) * CONV_W ** -0.5,
        "conv_b": 0.01 * nrm(ks[12], (DEPTH, CONV_DIM), jnp.float32),
        "a_log": jnp.log(jax.random.uniform(ks[13], (DEPTH, SSM_HEADS), minval=1.0, maxval=16.0)),
        "dt_bias": dt0 + jnp.log(-jnp.expm1(-dt0)),
        "d_skip": 1.0 + 0.1 * nrm(ks[14], (DEPTH, SSM_HEADS), jnp.float32),
        "ssm_norm_g": 1.0 + 0.02 * nrm(ks[15], (DEPTH, D_SSM), jnp.float32),
        "lam_qk": 0.1 * nrm(ks[16], (DEPTH, 4, DA_HEAD_DIM), jnp.float32),
        "subln_g": 1.0 + 0.02 * nrm(ks[17], (DEPTH, DA_V_DIM), jnp.float32),
        "w_out": nrm(ks[18], (DEPTH, MIX_WIDTH, D_MODEL), jnp.float32) * MIX_WIDTH ** -0.5,
        "norm_ffn_g": 1.0 + 0.02 * nrm(ks[19], (DEPTH, D_MODEL), jnp.float32),
        "peer_wq": nrm(ks[20], (DEPTH, D_MODEL, PEER_HEADS * PEER_DK), jnp.float32) * D_MODEL ** -0.5,
        "peer_sk1": nrm(ks[21], (DEPTH, PEER_HEADS, N_KEYS, PEER_DK // 2), jnp.float32) * (PEER_DK // 2) ** -0.5,
        "peer_sk2": nrm(ks[22], (DEPTH, PEER_HEADS, N_KEYS, PEER_DK // 2), jnp.float32) * (PEER_DK // 2) ** -0.5,
        "peer_u": nrm(ks[23], (DEPTH, N_EXPERTS, D_MODEL), jnp.float32) * D_MODEL ** -0.5,
        "peer_v": nrm(ks[24], (DEPTH, N_EXPERTS, D_MODEL), jnp.float32) * (PEER_HEADS * PEER_TOPK) ** -0.5,
        "norm_final_g": 1.0 + 0.02 * nrm(ks[25], (D_MODEL,), jnp.float32),
    }


def reference(x_prompt, x_sample, cache_k, cache_v, state_ssm, state_conv, page_table, rel_bias,
              norm_attn_g, w_in, conv_w, conv_b, a_log, dt_bias, d_skip, ssm_norm_g, lam_qk, subln_g,
              w_out, norm_ffn_g, peer_wq, peer_sk1, peer_sk2, peer_u, peer_v, norm_final_g):
    xp, xs = x_prompt, x_sample
    bp = x_prompt.shape[0]
    conv0 = jnp.zeros((bp, CONV_W - 1, CONV_DIM), x_prompt.dtype)
    ssm0 = jnp.zeros((bp, SSM_HEADS, SSM_HEAD_DIM, SSM_STATE), jnp.float32)
    kp_l, vp_l, hp_l, cp_l = [], [], [], []
    ks_l, vs_l, hs_l, cs_l = [], [], [], []
    for l in range(DEPTH):
        p = {"w_in": w_in[l], "conv_w": conv_w[l], "conv_b": conv_b[l], "a_log": a_log[l],
             "dt_bias": dt_bias[l], "d_skip": d_skip[l], "ssm_norm_g": ssm_norm_g[l],
             "lam_qk": lam_qk[l], "subln_g": subln_g[l], "w_out": w_out[l]}
        lam_init = 0.8 - 0.6 * math.exp(-0.3 * l)
        o, k_new, v_new, h_new, buf_new = mixer(rmsnorm(xp, norm_attn_g[l]), p, lam_init, rel_bias,
                                                conv0, ssm0, None, None, 0, SSD_CHUNK)
        xp = xp + o
        xp = xp + peer(rmsnorm(xp, norm_ffn_g[l]), peer_wq[l], peer_sk1[l], peer_sk2[l], peer_u[l], peer_v[l])
        kp_l.append(k_new); vp_l.append(v_new); hp_l.append(h_new); cp_l.append(buf_new)
        past_k = cache_k[l][page_table]
        past_v = cache_v[l][page_table]
        past_len = page_table.shape[1] * PAGE_SIZE
        o, k_new, v_new, h_new, buf_new = mixer(rmsnorm(xs, norm_attn_g[l]), p, lam_init, rel_bias,
                                                state_conv[l], state_ssm[l], past_k, past_v,
                                                past_len, xs.shape[1])
        xs = xs + o
        xs = xs + peer(rmsnorm(xs, norm_ffn_g[l]), peer_wq[l], peer_sk1[l], peer_sk2[l], peer_u[l], peer_v[l])
        ks_l.append(k_new); vs_l.append(v_new); hs_l.append(h_new); cs_l.append(buf_new)
    y_prompt = rmsnorm(xp, norm_final_g)
    y_sample = rmsnorm(xs, norm_final_g)
    return (y_prompt, y_sample,
            jnp.stack(kp_l), jnp.stack(vp_l), jnp.stack(hp_l), jnp.stack(cp_l),
            jnp.stack(ks_l), jnp.stack(vs_l), jnp.stack(hs_l), jnp.stack(cs_l))
```

```python
import math
import os
from contextlib import ExitStack
import numpy as np
import concourse.bass as bass
import concourse.mybir as mybir
from concourse.bass_utils import run_bass_kernel_spmd

F32 = mybir.dt.float32
BF16 = mybir.dt.bfloat16
I32 = mybir.dt.int32
AF = mybir.ActivationFunctionType
ALU = mybir.AluOpType
AX = mybir.AxisListType

FULL = dict(SEQ=2048, NPAGES=128, NPOOL=1280, NK=128, DEPTH=2)
D = 2048
DC = 16
EPS = 1e-6
NEG = -30000.0
N_DMA_SEMS = 40
NOPOOL = bool(int(os.environ.get('NOPOOL', '1')))
N_BG_SEMS = 16
N_P_SEMS = 16


class LazyTok:
    __slots__ = ("val",)

    def __init__(self):
        self.val = None


class Prog:
    def __init__(self, nc, es):
        self.nc = nc
        self.engs = {"pe": nc.tensor, "act": nc.scalar, "dve": nc.vector,
                     "pool": nc.gpsimd, "sp": nc.sync}
        self.ops = {k: [] for k in self.engs}
        self.cnt = {k: 0 for k in self.engs}
        self.known = {k: {} for k in self.engs}
        self.pending = {k: {} for k in self.engs}
        self.res = {}
        self.dma_slot = 0
        self.dma_val = [0] * N_DMA_SEMS
        self.out_tokens = []
        self.esem = {k: es.enter_context(nc.semaphore("e_" + k)) for k in self.engs}
        self.dsem = [es.enter_context(nc.semaphore("d_%d" % i)) for i in range(N_DMA_SEMS)]
        self.n_inst = 0
        self.dead = False
        self.nflush = 0
        self.pe_pending = []
        self.last_tok = None
        self.last_dma_tok = None
        self.last_eng_tok = None
        self.psum_keys = set()
        self.bgq = []
        self.bgn = 0
        self.bgkeys = {}
        self.bsem = [es.enter_context(nc.semaphore("b_%d" % i)) for i in range(N_BG_SEMS)]
        self.bg_slot = 0
        self.bg_val = [0] * N_BG_SEMS
        self.psem = [es.enter_context(nc.semaphore("p_%d" % i)) for i in range(N_P_SEMS)]
        self.p_slot = 0
        self.p_val = [0] * N_P_SEMS

    def _st(self, key):
        s = self.res.get(key)
        if s is None:
            s = {"w": None, "r": []}
            self.res[key] = s
        return s

    def _pe_resolve(self):
        if not self.pe_pending:
            return
        rec = self.ops["pe"][-1]
        assert rec[3] is False
        self.cnt["pe"] += 1
        rec[3] = True
        for t in self.pe_pending:
            t.val = self.cnt["pe"]
        self.pe_pending = []
        self.last_tok = None
        self.last_dma_tok = None
        self.last_eng_tok = None
        self.psum_keys = set()

    def _need(self, eng, tok, waits):
        if tok is None:
            return
        if isinstance(tok, LazyTok):
            if eng == "pe":
                return
            if tok.val is None:
                self._pe_resolve()
            tok = (("e", "pe"), tok.val)
        sem, val = tok
        if sem == ("e", eng) and eng == "pe":
            return
        if self.known[eng].get(sem, 0) >= val:
            return
        waits[sem] = max(waits.get(sem, 0), val)

    def barrier(self):
        snap = {("e", k): v for k, v in self.cnt.items() if v > 0}
        for i, v in enumerate(self.dma_val):
            if v > 0:
                snap[("d", i)] = v
        for i, v in enumerate(self.p_val):
            if v > 0:
                snap[("p", i)] = v
        for k in self.engs:
            self.pending[k] = dict(snap)

    def op(self, eng, fn, reads=(), writes=(), dma=False, is_out=False, bg=False):
        if self.dead:
            return None
        import os
        mo = int(os.environ.get("MAXOPS", "0"))
        if mo and self.n_inst >= mo and not is_out:
            return None
        sk = os.environ.get("SKIPOPS", "")
        if sk:
            a_, b_ = sk.split("-")
            if int(a_) <= self.n_inst <= int(b_):
                self.n_inst += 1
                return None
        waits = {}
        for sem, val in self.pending[eng].items():
            if sem == ("e", eng):
                continue
            self._need(eng, (sem, val), waits)
        self.pending[eng] = {}
        if eng == "pool" and not dma and NOPOOL and self.nflush > 0:
            eng = "dve"
        pr = [r for r in reads if r in self.psum_keys]
        if pr:
            writes = list(writes) + [r for r in pr if r not in writes]
        if os.environ.get("PSLOCK") and (pr or [w for w in writes if w in self.psum_keys]):
            writes = list(writes) + ["__pslock__"]
        smode = os.environ.get("SERIAL", "1")
        if smode == "1" and self.last_tok is not None:
            self._need(eng, self.last_tok, waits)
        if smode == "dma" and self.last_dma_tok is not None:
            self._need(eng, self.last_dma_tok, waits)
        if smode == "eng" and self.last_eng_tok is not None and not dma:
            self._need(eng, self.last_eng_tok, waits)
        for r in reads:
            self._need(eng, self._st(r)["w"], waits)
        for w in writes:
            st = self._st(w)
            self._need(eng, st["w"], waits)
            for t in st["r"]:
                self._need(eng, t, waits)
        if dma and bg:
            slot = self.bg_slot
            self.bg_slot = (self.bg_slot + 1) % N_BG_SEMS
            sem = ("b", slot)
            if self.bg_val[slot] > 0:
                self._need(eng, (sem, self.bg_val[slot]), waits)
            self.bg_val[slot] += 16
            tok = (sem, self.bg_val[slot])
        elif dma and eng == "pool":
            slot = self.p_slot
            self.p_slot = (self.p_slot + 1) % N_P_SEMS
            sem = ("p", slot)
            if self.p_val[slot] > 0:
                self._need(eng, (sem, self.p_val[slot]), waits)
            self.p_val[slot] += 16
            tok = (sem, self.p_val[slot])
        elif dma:
            assert eng == "sp"
            slot = self.dma_slot
            self.dma_slot = (self.dma_slot + 1) % N_DMA_SEMS
            sem = ("d", slot)
            if self.dma_val[slot] > 0:
                self._need(eng, (sem, self.dma_val[slot]), waits)
            self.dma_val[slot] += 16
            tok = (sem, self.dma_val[slot])
        elif eng == "pe":
            tok = LazyTok()
        else:
            self.cnt[eng] += 1
            tok = (("e", eng), self.cnt[eng])
        for sem, val in waits.items():
            self.known[eng][sem] = max(self.known[eng].get(sem, 0), val)
        if eng == "pe":
            self.ops[eng].append([list(waits.items()), fn, tok, False])
            self.pe_pending.append(tok)
        else:
            self.ops[eng].append((list(waits.items()), fn, tok))
        self.n_inst += 1
        for r in reads:
            self._st(r)["r"].append(tok)
        for w in writes:
            st = self._st(w)
            st["w"] = tok
            st["r"] = []
        if is_out:
            self.out_tokens.append(tok)
        if not bg:
            self.last_tok = tok
            if dma:
                self.last_dma_tok = tok
            else:
                self.last_eng_tok = tok
        return tok

    def S(self, sem):
        if sem[0] == "e":
            return self.esem[sem[1]]
        if sem[0] == "p":
            return self.psem[sem[1]]
        return self.dsem[sem[1]] if sem[0] == "d" else self.bsem[sem[1]]

    def flush(self, final=False):
        if self.dead:
            return
        self.nflush += 1
        self._pe_resolve()
        import os
        mf = int(os.environ.get("MAXFLUSH", "0"))
        if mf and self.nflush >= mf:
            final = True
        self._flush(final)
        if mf and self.nflush >= mf:
            self.dead = True

    def _flush(self, final=False):
        nc = self.nc
        fin = {}
        if final:
            for sem, val in self.out_tokens:
                fin[sem] = max(fin.get(sem, 0), val)
            for i, v in enumerate(self.dma_val):
                if v > 0:
                    fin[("d", i)] = v
            for i, v in enumerate(self.bg_val):
                if v > 0:
                    fin[("b", i)] = v
            for i, v in enumerate(self.p_val):
                if v > 0:
                    fin[("p", i)] = v
            for k, v in self.cnt.items():
                if v > 0 and k != "sp":
                    fin[("e", k)] = v
        with nc.Block() as block:
            def mk(k):
                ops = self.ops[k]

                def body(e):
                    for rec in ops:
                        waits, fn, tok = rec[0], rec[1], rec[2]
                        for sem, val in waits:
                            e.wait_ge(self.S(sem), val)
                        inst = fn(e)
                        if isinstance(tok, LazyTok):
                            if rec[3]:
                                inst.then_inc(self.esem["pe"], 1)
                            continue
                        inst.then_inc(self.S(tok[0]), 1 if tok[0][0] == "e" else 16)
                    if k == "sp" and final:
                        for sem, val in fin.items():
                            e.wait_ge(self.S(sem), val)
                return body
            reg = {"sp": block.sync, "pe": block.tensor, "act": block.scalar,
                   "dve": block.vector, "pool": block.gpsimd}
            for k in self.engs:
                if self.ops[k] or (k == "sp" and final):
                    reg[k](mk(k))
        self.ops = {k: [] for k in self.engs}
        self.barrier()

    def dma(self, eng, out, in_, r=(), w=(), is_out=False, nc_ok=False):
        if nc_ok:
            return self.op(eng, lambda e: e.dma_start(out=out, in_=in_, allow_slow_non_contiguous=True), r, w, dma=True, is_out=is_out)
        return self.op(eng, lambda e: e.dma_start(out=out, in_=in_), r, w, dma=True, is_out=is_out)

    def mm(self, out, lhsT, rhs, r, w, start=True, stop=True):
        return self.op("pe", lambda e: e.matmul(out, lhsT=lhsT, rhs=rhs, start=start, stop=stop), r, w)

    def tr(self, out, in_, ident, r, w):
        return self.op("pe", lambda e: e.transpose(out=out, in_=in_, identity=ident), r, w)

    def act(self, out, in_, func, r, w, **kw):
        return self.op("act", lambda e: e.activation(out=out, in_=in_, func=func, **kw), r, w)

    def copy(self, eng, out, in_, r, w):
        if eng == "act":
            return self.op("act", lambda e: e.copy(out=out, in_=in_), r, w)
        return self.op(eng, lambda e: e.tensor_copy(out=out, in_=in_), r, w)

    def tt(self, eng, out, in0, in1, op, r, w):
        return self.op(eng, lambda e: e.tensor_tensor(out=out, in0=in0, in1=in1, op=op), r, w)

    def ts(self, eng, out, in0, s1, op0, r, w, s2=None, op1=None):
        if op1 is None:
            return self.op(eng, lambda e: e.tensor_scalar(out=out, in0=in0, scalar1=s1, scalar2=None, op0=op0), r, w)
        return self.op(eng, lambda e: e.tensor_scalar(out=out, in0=in0, scalar1=s1, scalar2=s2, op0=op0, op1=op1), r, w)

    def stt(self, eng, out, in0, scalar, in1, op0, op1, r, w):
        return self.op(eng, lambda e: e.scalar_tensor_tensor(out=out, in0=in0, scalar=scalar, in1=in1, op0=op0, op1=op1), r, w)

    def memset(self, eng, out, val, w):
        return self.op(eng, lambda e: e.memset(out, val), (), w)

    def recip(self, out, in_, r, w):
        return self.op("dve", lambda e: e.reciprocal(out=out, in_=in_), r, w)

    def vmax(self, out, in_, r, w):
        return self.op("dve", lambda e: e.max(out=out, in_=in_), r, w)

    def mrep(self, out, rep, vals, r, w):
        return self.op("dve", lambda e: e.match_replace(out=out, in_to_replace=rep, in_values=vals, imm_value=-1e30), r, w)

    def gather(self, out, table, idx, r, w):
        return self.op("pool", lambda e: e.indirect_dma_start(out=out, out_offset=None, in_=table,
                       in_offset=bass.IndirectOffsetOnAxis(ap=idx, axis=0)), r, w, dma=True)

    def bg_add(self, out, in_, key):
        self.bgq.append((out, in_, key))

    def pump(self, k=1):
        for _ in range(k):
            if not self.bgq:
                return
            out, in_, key = self.bgq.pop(0)
            self.op("pool", (lambda o, i: (lambda e: e.dma_start(out=o, in_=i)))(out, in_), (), [key + "_%d" % self.bgn], dma=True, bg=True)
            self.bgkeys.setdefault(key, []).append(key + "_%d" % self.bgn)
            self.bgn += 1

    def rsum(self, out, in_, r, w):
        return self.op("dve", lambda e: e.reduce_sum(out=out, in_=in_, axis=AX.X), r, w)


class Ctx:
    pass


def rel_bucket_np(rel):
    n = np.maximum(rel, 0)
    max_exact = 16
    nf = np.maximum(n, 1).astype(np.float32)
    large = max_exact + (np.log(nf / max_exact) / math.log(128 / max_exact) * (32 - max_exact)).astype(np.int32)
    large = np.minimum(large, 31)
    return np.where(n < max_exact, n, large)


def host_consts():
    k = np.arange(128)[:, None]
    q = np.arange(128)[None, :]
    bk = np.zeros((128, 264), np.float32)
    mk = np.zeros((128, 264), np.float32)
    bk[:, 0:128] = rel_bucket_np(128 + q - k)
    bk[:, 128:256] = rel_bucket_np(q - k)
    mk[:, 128:256] = np.where(k > q, NEG, 0.0)
    qi = np.arange(4)[None, :]
    bk[:, 256:260] = rel_bucket_np(128 + qi - k)
    kj = np.arange(128)[:, None]
    bk[:, 260:264] = rel_bucket_np(qi - kj)
    mk[:, 260:264] = np.where(kj > qi, NEG, 0.0)
    sel = np.zeros((8, 4), np.float32)
    for m in range(2):
        for qq in range(4):
            sel[m * 4 + qq, qq] = 1.0
    cm = np.zeros((8, 2), np.float32)
    cm[0:4, 0] = 1.0
    cm[4:8, 1] = 1.0
    return bk, mk, sel, cm


def build(cfg, dbg=False):
    SEQ, NPAGES, NPOOL, NK, DEPTH = cfg["SEQ"], cfg["NPAGES"], cfg["NPOOL"], cfg["NK"], cfg["DEPTH"]
    NE = NK * NK
    NT = SEQ // 128
    NTOK = SEQ + 4
    tiles = [(t, t * 128, 128) for t in range(NT)] + [(NT, SEQ, 4)]
    groups = [(s, min(512, SEQ - s)) for s in range(0, SEQ, 512)] + [(SEQ, 4)]
    nc = bass.Bass("TRN2", target_bir_lowering=False)
    es = ExitStack()
    P = Prog(nc, es)

    def din(name, shape, dt=F32):
        return nc.dram_tensor(name, list(shape), dt, kind="ExternalInput").ap()

    def dout(name, shape, dt=F32):
        return nc.dram_tensor(name, list(shape), dt, kind="ExternalOutput").ap()

    def dscr(name, shape, dt=F32):
        return nc.dram_tensor(name, list(shape), dt, kind="ExternalOutput" if dbg else "Internal").ap()

    xp = din("xp", [SEQ, D]); xs = din("xs", [4, D])
    cache_k = din("cache_k", [DEPTH * NPOOL * 128, 1024]); cache_v = din("cache_v", [DEPTH * NPOOL * 128, 1024])
    state_ssm = din("state_ssm", [DEPTH, 16, 64, 128]); state_conv = din("state_conv", [DEPTH, 3, 1536])
    page_table = din("page_table", [1, NPAGES], I32)
    rel_bias = din("rel_bias", [1, 256])
    norm_attn_g = din("norm_attn_g", [DEPTH, D]); w_in = din("w_in", [DEPTH, D, 5648])
    conv_w = din("conv_w", [DEPTH, 4, 1536]); conv_b = din("conv_b", [DEPTH, 1536])
    a_log = din("a_log", [DEPTH, 16]); dt_bias = din("dt_bias", [DEPTH, 16]); d_skip = din("d_skip", [DEPTH, 16])
    ssm_norm_g = din("ssm_norm_g", [DEPTH, 1024]); lam_qk = din("lam_qk", [1, DEPTH * 256]); subln_g = din("subln_g", [DEPTH, 128])
    w_out = din("w_out", [DEPTH, D, D]); norm_ffn_g = din("norm_ffn_g", [DEPTH, D]); peer_wq = din("peer_wq", [DEPTH, D, D])
    sk1T = din("sk1T", [DEPTH, 8, 128, NK]); sk2T = din("sk2T", [DEPTH, 8, 128, NK])
    peer_uT = din("peer_uT", [DEPTH, D, NE]); peer_v = din("peer_v", [DEPTH, NE, D])
    norm_final_g = din("norm_final_g", [1, D])
    c_bk = din("c_bk", [128, 264]); c_mk = din("c_mk", [128, 264]); c_sel = din("c_sel", [8, 4]); c_cm = din("c_cm", [8, 2])

    y_prompt = dout("y_prompt", [SEQ, D]); y_sample = dout("y_sample", [4, D])
    k_prompt = dout("k_prompt", [DEPTH, SEQ, 1024]); v_prompt = dout("v_prompt", [DEPTH, SEQ, 1024])
    ssm_prompt = dout("ssm_prompt", [DEPTH, 16, 64, 128]); conv_prompt = dout("conv_prompt", [DEPTH, 3, 1536])
    k_sample = dout("k_sample", [DEPTH, 4, 1024]); v_sample = dout("v_sample", [DEPTH, 4, 1024])
    ssm_sample = dout("ssm_sample", [DEPTH, 16, 64, 128]); conv_sample = dout("conv_sample", [DEPTH, 3, 1536])

    X = dscr("X", [NTOK, D]); ZS = dscr("ZS", [NTOK, 1024]); XBCT = dscr("XBCT", [1536, NTOK])
    QT = dscr("QT", [1024, NTOK], BF16); KT = dscr("KT", [1024, NTOK], BF16)
    N2T = dscr("N2T", [D, NTOK], BF16); SC = dscr("SC", [NTOK, 16 * NK])
    UT16 = nc.dram_tensor("UT16", [DEPTH, D, NE], BF16, kind="Internal").ap()
    V16 = nc.dram_tensor("V16", [DEPTH, NE, D], BF16, kind="Internal").ap()

    uid = [0]

    def sb(name, shape, dt=F32, stack=None):
        uid[0] += 1
        return (stack or es).enter_context(nc.sbuf_tensor("%s_u%d" % (name, uid[0]), list(shape), dt))

    def ps(name, shape, dt=F32, stack=None):
        uid[0] += 1
        P.psum_keys.add(name)
        isz = 4 if dt == F32 else 2
        n = 1
        for d_ in shape[1:]:
            n *= d_
        per_bank = 2048 // isz
        tot = ((n + per_bank - 1) // per_bank) * per_bank
        full = (stack or es).enter_context(nc.psum_tensor("%s_u%d" % (name, uid[0]), [shape[0], tot], dt))
        v = full[:, 0:n]
        if len(shape) == 3:
            v = v.rearrange("p (a b) -> p a b", a=shape[1])
        elif len(shape) == 4:
            v = v.rearrange("p (a b c) -> p a b c", a=shape[1], b=shape[2])
        return v

    identf = sb("identf", [128, 128]); identb = sb("identb", [128, 128], BF16)
    tri = sb("tri", [128, 128]); umat = sb("umat", [128, 128]); ones = sb("ones", [128, 128])
    Tb = sb("Tb", [128, 8, 264])
    lamt = sb("lamt", [128, DEPTH]); nlam = sb("nlam", [128, DEPTH])
    TAU = sb("TAU", [128, NT + 1, 8]); NB = sb("NB", [128, NT + 1, 8])
    selm = sb("selm", [8, 4]); cmm = sb("cmm", [8, 2])
    idx_all = sb("idx_all", [128, DEPTH, NPAGES], I32)
    DTs = sb("DTs", [128, NT + 1, 16], F32)

    with ExitStack() as ph:
        io = sb("io", [128, 128], F32, ph); bk = sb("bk", [128, 264], F32, ph); mk = sb("mk", [128, 264], F32, ph)
        rbb = sb("rbb", [128, 32, 8], F32, ph); rbd = sb("rbd", [128, 32, 8], F32, ph); tmpb = sb("tmpb", [128, 8, 264], F32, ph)
        lq = sb("lq", [128, DEPTH, 2, 2, 64], F32, ph); lqp = sb("lqp", [128, DEPTH, 2, 64], F32, ph); lqs = sb("lqs", [128, DEPTH, 2], F32, ph)
        ptb = sb("ptb", [128, NPAGES], I32, ph); iop = sb("iop", [128, NPAGES], I32, ph)
        P.op("pool", lambda e: e.iota(io[:], pattern=[[1, 128]], base=0, channel_multiplier=-1, allow_small_or_imprecise_dtypes=True), (), ["io"])
        P.op("pool", lambda e: e.iota(iop[:], pattern=[[0, NPAGES]], base=0, channel_multiplier=1), (), ["iop"])
        P.op("dve", lambda e: e.tensor_single_scalar(out=identf[:], in_=io[:], scalar=0.0, op=ALU.is_equal), ["io"], ["identf"])
        P.op("dve", lambda e: e.tensor_single_scalar(out=tri[:], in_=io[:], scalar=0.0, op=ALU.is_ge), ["io"], ["tri"])
        P.op("dve", lambda e: e.tensor_single_scalar(out=umat[:], in_=io[:], scalar=0.0, op=ALU.is_lt), ["io"], ["umat"])
        P.copy("dve", identb[:], identf[:], ["identf"], ["identb"])
        P.memset("dve", ones[:], 1.0, ["ones"])
        P.memset("dve", DTs[:], 0.0, ["DTs"])
        P.dma("sp", bk[:], c_bk, (), ["bk"]); P.dma("sp", mk[:], c_mk, (), ["mk"])
        P.dma("sp", selm[:], c_sel, (), ["selm"]); P.dma("sp", cmm[:], c_cm, (), ["cmm"])
        P.dma("sp", rbb[:].rearrange("p a b -> p (a b)"), rel_bias[0:1, :].to_broadcast([128, 256]), (), ["rbb"])
        P.tt("dve", rbd[:], rbb[:], rbb[:, 31:32, :].to_broadcast([128, 32, 8]), ALU.subtract, ["rbb"], ["rbd"])
        P.memset("dve", Tb[:], 0.0, ["Tb"])
        for b in range(31):
            P.stt("dve", tmpb[:], bk[:].unsqueeze(1).to_broadcast([128, 8, 264]), float(b),
                  rbd[:, b, :].unsqueeze(2).to_broadcast([128, 8, 264]), ALU.is_equal, ALU.mult, ["bk", "rbd"], ["tmpb"])
            P.tt("pool", Tb[:], Tb[:], tmpb[:], ALU.add, ["tmpb", "Tb"], ["Tb"])
        P.tt("dve", Tb[:], Tb[:], mk[:].unsqueeze(1).to_broadcast([128, 8, 264]), ALU.add, ["Tb", "mk"], ["Tb"])
        P.dma("sp", lq[:].rearrange("p l a b d -> p (l a b d)"), lam_qk[0:1, :].to_broadcast([128, DEPTH * 256]), (), ["lq"])
        for l in range(DEPTH):
            P.tt("dve", lqp[:, l], lq[:, l, :, 0, :], lq[:, l, :, 1, :], ALU.mult, ["lq"], ["lqp"])
            P.rsum(lqs[:, l, :], lqp[:, l], ["lqp"], ["lqs"])
        P.act(lqs[:], lqs[:], AF.Exp, ["lqs"], ["lqs"])
        for l in range(DEPTH):
            li = 0.8 - 0.6 * math.exp(-0.3 * l)
            P.stt("dve", lamt[:, l:l + 1], lqs[:, l, 0:1], li, lqs[:, l, 1:2], ALU.add, ALU.subtract, ["lqs"], ["lamt"])
        P.ts("dve", nlam[:], lamt[:], -1.0, ALU.mult, ["lamt"], ["nlam"])
        P.dma("sp", ptb[:], page_table[0:1, :].to_broadcast([128, NPAGES]), (), ["ptb"])
        ptf = sb("ptf", [128, NPAGES], F32, ph); iof = sb("iof", [128, NPAGES], F32, ph); idf = sb("idf", [128, NPAGES], F32, ph)
        P.copy("dve", ptf[:], ptb[:], ["ptb"], ["ptf"])
        P.copy("dve", iof[:], iop[:], ["iop"], ["iof"])
        for l in range(DEPTH):
            P.stt("dve", idf[:], ptf[:], 128.0, iof[:], ALU.mult, ALU.add, ["ptf", "iof"], ["idf"])
            if l > 0:
                P.ts("dve", idf[:], idf[:], float(l * NPOOL * 128), ALU.add, ["idf"], ["idf"])
            P.copy("dve", idx_all[:, l, :], idf[:], ["idf"], ["idx"])
        P.flush()

    rr = {"i": 0}

    def evac_eng():
        rr["i"] += 1
        return "act" if rr["i"] % 2 else "dve"

    def load_bcast(ph, name, row_ap, n):
        t = sb(name, [128, n], F32, ph)
        P.dma("sp", t[:], row_ap.to_broadcast([128, n]), (), [name])
        return t

    def phase_norm(src_of_tile, g_row, dstT=None, dstT_dram=None, out_of_tile=None, tag="n"):
        with ExitStack() as ph:
            gb = load_bcast(ph, tag + "gb", g_row, D)
            xt = [sb(tag + "xt%d" % i, [128, D], F32, ph) for i in range(2)]
            xn = [sb(tag + "xn%d" % i, [128, D], BF16 if out_of_tile is None else F32, ph) for i in range(2)]
            junk = sb(tag + "junk", [128, D], BF16, ph)
            ss = [sb(tag + "ss%d" % i, [128, 4], F32, ph) for i in range(2)]
            if out_of_tile is None:
                tp = [ps(tag + "tp%d" % i, [128, DC, 128], BF16, ph) for i in range(2)]
                stg = [sb(tag + "stg%d" % i, [128, DC, 128], BF16, ph) for i in range(2)]
            for (t, t0, n) in tiles:
                b = t % 2
                P.dma("sp", xt[b][:n], src_of_tile(t, t0, n), (), [tag + "xt%d" % b])
                P.memset("pool", ss[b][:n, 0:1], 0.0, [tag + "ss%d" % b])
                P.act(junk[:n], xt[b][:n], AF.Square, [tag + "xt%d" % b, tag + "ss%d" % b], [tag + "junk", tag + "ss%d" % b], accum_out=ss[b][:n, 0:1])
                P.act(ss[b][:n, 1:2], ss[b][:n, 0:1], AF.Sqrt, [tag + "ss%d" % b], [tag + "ss%d" % b], scale=1.0 / D, bias=EPS)
                P.recip(ss[b][:n, 2:3], ss[b][:n, 1:2], [tag + "ss%d" % b], [tag + "ss%d" % b])
                P.stt("dve", xn[b][:n], xt[b][:n], ss[b][:n, 2:3], gb[:n], ALU.mult, ALU.mult,
                      [tag + "xt%d" % b, tag + "ss%d" % b, tag + "gb"], [tag + "xn%d" % b])
                if out_of_tile is not None:
                    P.dma("sp", out_of_tile(t, t0, n), xn[b][:n], [tag + "xn%d" % b], (), is_out=True)
                    continue
                for dc in range(DC):
                    P.tr(tp[b][:, dc, :n], xn[b][:n, dc * 128:(dc + 1) * 128], identb[:n, :n], [tag + "xn%d" % b, "identb"], [tag + "tp%d" % b])
                if dstT is not None:
                    P.copy(evac_eng(), dstT[:, :, t0:t0 + n], tp[b][:, :, :n], [tag + "tp%d" % b], ["nT"])
                else:
                    P.copy(evac_eng(), stg[b][:, :, :n], tp[b][:, :, :n], [tag + "tp%d" % b], [tag + "stg%d" % b])
                    P.dma("sp", dstT_dram[:, t0:t0 + n].rearrange("(c p) t -> p c t", p=128), stg[b][:, :, :n], [tag + "stg%d" % b], ())
            P.flush()

    def phase_proj(W_dram, nT, specs, tag="pj"):
        with ExitStack() as ph:
            W16 = [sb(tag + "W%d" % i, [128, DC, 512], BF16, ph) for i in range(2)]
            pp = [ps(tag + "pp%d" % i, [128, 512], F32, ph) for i in range(3)]
            ctx = Ctx(); ctx.ph = ph
            k = 0
            for ci, (c0, w, mode, sink) in enumerate(specs):
                b = ci % 2
                wk = tag + "W%d" % b
                P.dma("pool", W16[b][:, :, :w], W_dram[:, c0:c0 + w].rearrange("(c p) w -> p c w", p=128), (), [wk])
                if mode == "T":
                    for (t, t0, n) in tiles:
                        pb = k % 3; k += 1
                        for dc in range(DC):
                            P.mm(pp[pb][:n, :w], nT[:, dc, t0:t0 + n], W16[b][:, dc, :w], ["nT", wk], [tag + "pp%d" % pb], start=(dc == 0), stop=(dc == DC - 1))
                        sink(t, t0, n, c0, w, pp[pb][:n, :w], tag + "pp%d" % pb)
                else:
                    for sub in range(w // 128):
                        for (g0, gn) in groups:
                            pb = k % 3; k += 1
                            for dc in range(DC):
                                P.mm(pp[pb][:, :gn], W16[b][:, dc, sub * 128:(sub + 1) * 128], nT[:, dc, g0:g0 + gn], ["nT", wk], [tag + "pp%d" % pb], start=(dc == 0), stop=(dc == DC - 1))
                            sink(c0 + sub * 128, g0, gn, pp[pb][:, :gn], tag + "pp%d" % pb)
            P.flush()

    class Stage:
        def __init__(self, ph, name, shape, dt, nbuf):
            self.bufs = [sb("%s%d" % (name, i), shape, dt, ph) for i in range(nbuf)]
            self.keys = ["%s%d" % (name, i) for i in range(nbuf)]
            self.i = 0

        def next(self):
            self.i = (self.i + 1) % len(self.bufs)
            return self.bufs[self.i], self.keys[self.i]

    def x_src(l):
        if l == 0:
            return lambda t, t0, n: (xp[t0:t0 + n, :] if t0 < SEQ else xs[0:4, :])
        return lambda t, t0, n: X[t0:t0 + n, :]

    for l in range(DEPTH):
        lam_init = 0.8 - 0.6 * math.exp(-0.3 * l)
        with ExitStack() as L1:
            nT = sb("nT", [128, DC, NTOK], BF16, L1)
            phase_norm(x_src(l), norm_attn_g[l:l + 1, :], dstT=nT, tag="na")
            with ExitStack() as ph:
                s32 = Stage(ph, "s32_", [128, 512], F32, 4)
                s16 = Stage(ph, "s16_", [128, 512], BF16, 3)

                def sink_z(t, t0, n, c0, w, pap, pk):
                    st, sk = s32.next()
                    P.act(st[:n, :w], pap, AF.Silu, [pk], [sk])
                    P.dma("sp", ZS[t0:t0 + n, c0:c0 + w], st[:n, :w], [sk], ())

                def sink_xbc(cs, g0, gn, pap, pk):
                    st, sk = s32.next()
                    P.copy(evac_eng(), st[:, :gn], pap, [pk], [sk])
                    P.dma("sp", XBCT[cs - 1024:cs - 1024 + 128, g0:g0 + gn], st[:, :gn], [sk], ())

                def sink_dt(t, t0, n, c0, w, pap, pk):
                    P.copy("dve", DTs[:n, t, :], pap, [pk], ["DTs"])

                def mk_sink_f16(dst, base):
                    def f(cs, g0, gn, pap, pk):
                        st, sk = s16.next()
                        P.copy(evac_eng(), st[:, :gn], pap, [pk], [sk])
                        P.dma("sp", dst[cs - base:cs - base + 128, g0:g0 + gn], st[:, :gn], [sk], ())
                    return f

                def mk_sink_tok(dp, dsm, base, key):
                    def f(t, t0, n, c0, w, pap, pk):
                        st, sk = s32.next()
                        P.copy(evac_eng(), st[:n, :w], pap, [pk], [sk])
                        if t0 < SEQ:
                            P.dma("sp", dp[l, t0:t0 + n, c0 - base:c0 - base + w], st[:n, :w], [sk], (), is_out=True)
                        else:
                            P.dma("sp", dsm[l, 0:4, c0 - base:c0 - base + w], st[:n, :w], [sk], (), is_out=True)
                    return f

                specs = [(0, 512, "T", sink_z), (512, 512, "T", sink_z),
                         (1024, 512, "F", sink_xbc), (1536, 512, "F", sink_xbc), (2048, 512, "F", sink_xbc),
                         (2560, 16, "T", sink_dt),
                         (2576, 512, "F", mk_sink_f16(QT, 2576)), (3088, 512, "F", mk_sink_f16(QT, 2576)),
                         (3600, 512, "F", mk_sink_f16(KT, 3600)), (4112, 512, "F", mk_sink_f16(KT, 3600)),
                         (3600, 512, "T", mk_sink_tok(k_prompt, k_sample, 3600, "kout")), (4112, 512, "T", mk_sink_tok(k_prompt, k_sample, 3600, "kout")),
                         (4624, 512, "T", mk_sink_tok(v_prompt, v_sample, 4624, "vout")), (5136, 512, "T", mk_sink_tok(v_prompt, v_sample, 4624, "vout"))]
                phase_proj(w_in[l], nT, specs)
        RB = 1024
        for r0 in range(0, D, RB):
            for c0 in range(0, NE, 2048):
                cw = min(2048, NE - c0)
                P.bg_add(UT16[l, r0:r0 + RB, c0:c0 + cw], peer_uT[l, r0:r0 + RB, c0:c0 + cw], "UT16_%d" % l)
        for r0 in range(0, NE, RB):
            P.bg_add(V16[l, r0:r0 + RB, :], peer_v[l, r0:r0 + RB, :], "V16_%d" % l)

        with ExitStack() as LM:
            YOT = sb("YOT", [128, DC, NTOK], BF16, LM)
            with ExitStack() as ph:
                Xtok = sb("Xtok", [128, NT + 1, 1024], BF16, ph)
                Btok = sb("Btok", [128, NT + 1, 2, 128], BF16, ph)
                BT = sb("BT", [128, 2, NTOK], BF16, ph); CT = sb("CT", [128, 2, NTOK], BF16, ph)
                cwt = sb("cwt", [128, 4, 12], F32, ph); cbt = sb("cbt", [128, 12], F32, ph)
                dtb = load_bcast(ph, "dtb", dt_bias[l:l + 1, :], 16)
                alb = load_bcast(ph, "alb", a_log[l:l + 1, :], 16)
                dsk = load_bcast(ph, "dsk", d_skip[l:l + 1, :], 16)
                sng = load_bcast(ph, "sng", ssm_norm_g[l:l + 1, :], 1024)
                dtv = sb("dtv", [128, NT + 1, 16], F32, ph); dtA = sb("dtA", [128, NT + 1, 16], F32, ph)
                t1 = sb("dt_t1", [128, NT + 1, 16], F32, ph); t2 = sb("dt_t2", [128, NT + 1, 16], F32, ph)
                for kk in range(4):
                    P.dma("sp", cwt[:, kk, :], conv_w[l, kk:kk + 1, :].rearrange("o (c p) -> p (o c)", p=128), (), ["cwt"], nc_ok=True)
                P.dma("sp", cbt[:], conv_b[l:l + 1, :].rearrange("o (c p) -> p (o c)", p=128), (), ["cbt"], nc_ok=True)
                P.act(alb[:], alb[:], AF.Exp, ["alb"], ["alb"])
                P.ts("dve", alb[:], alb[:], -1.0, ALU.mult, ["alb"], ["alb"])
                P.tt("dve", t1[:], DTs[:], dtb[:].unsqueeze(1).to_broadcast([128, NT + 1, 16]), ALU.add, ["DTs", "dtb"], ["t1"])
                P.ts("dve", t2[:], t1[:], -1.0, ALU.mult, ["t1"], ["t2"])
                P.tt("dve", t2[:], t2[:], t1[:], ALU.min, ["t1", "t2"], ["t2"])
                P.act(t2[:], t2[:], AF.Exp, ["t2"], ["t2"])
                P.act(t2[:], t2[:], AF.Ln, ["t2"], ["t2"], bias=1.0)
                P.stt("dve", dtv[:], t1[:], 0.0, t2[:], ALU.max, ALU.add, ["t1", "t2"], ["dtv"])
                P.tt("dve", dtA[:], dtv[:], alb[:].unsqueeze(1).to_broadcast([128, NT + 1, 16]), ALU.mult, ["dtv", "alb"], ["dtA"])
                with ExitStack() as cv:
                    xin = [sb("xin%d" % i, [128, 3 + SEQ], F32, cv) for i in range(2)]
                    xsi = [sb("xsi%d" % i, [128, 8], F32, cv) for i in range(2)]
                    cacc = sb("cacc", [128, SEQ], F32, cv); cacs = sb("cacs", [128, 4], F32, cv)
                    xc16 = [sb("xc16_%d" % i, [128, NTOK], BF16, cv) for i in range(2)]
                    ctp = [ps("ctp%d" % i, [128, 4, 128], BF16, cv) for i in range(2)]
                    ctk = 0
                    for cc in range(12):
                        b = cc % 2
                        ch = slice(cc * 128, (cc + 1) * 128)
                        P.memset("pool", xin[b][:, 0:3], 0.0, ["xin%d" % b])
                        P.dma("sp", xin[b][:, 3:3 + SEQ], XBCT[ch, 0:SEQ], ["XBCT"], ["xin%d" % b])
                        P.dma("sp", xsi[b][:, 0:3], state_conv[l, :, ch].rearrange("k c -> c k"), (), ["xsi%d" % b], nc_ok=True)
                        P.dma("sp", xsi[b][:, 3:7], XBCT[ch, SEQ:SEQ + 4], ["XBCT"], ["xsi%d" % b])
                        P.dma("sp", conv_prompt[l, :, ch].rearrange("k c -> c k"), xin[b][:, SEQ:SEQ + 3], ["xin%d" % b], (), is_out=True, nc_ok=True)
                        P.dma("sp", conv_sample[l, :, ch].rearrange("k c -> c k"), xsi[b][:, 4:7], ["xsi%d" % b], (), is_out=True, nc_ok=True)
                        for (src, srck, acc, acck, L, o0) in ((xin[b], "xin%d" % b, cacc, "cacc", SEQ, 0), (xsi[b], "xsi%d" % b, cacs, "cacs", 4, SEQ)):
                            P.ts("dve", acc[:, :L], src[:, 0:L], cwt[:, 0, cc:cc + 1], ALU.mult, [srck, "cwt"], [acck])
                            for kk in range(1, 4):
                                P.stt("dve", acc[:, :L], src[:, kk:kk + L], cwt[:, kk, cc:cc + 1], acc[:, :L], ALU.mult, ALU.add, [srck, "cwt", acck], [acck])
                            dst = xc16[b][:, o0:o0 + L] if cc < 8 else (BT[:, cc - 8, o0:o0 + L] if cc < 10 else CT[:, cc - 10, o0:o0 + L])
                            dk = "xc16_%d" % b if cc < 8 else ("BT" if cc < 10 else "CT")
                            P.act(dst, acc[:, :L], AF.Silu, [acck, "cbt"], [dk], bias=cbt[:, cc:cc + 1])
                        if cc >= 10:
                            continue
                        srcT = xc16[b] if cc < 8 else BT[:, cc - 8, :]
                        sk_ = "xc16_%d" % b if cc < 8 else "BT"
                        for t4 in range(0, NT + 1, 4):
                            pb = ctk % 2; ctk += 1
                            grp = tiles[t4:t4 + 4]
                            for i, (t, t0, n) in enumerate(grp):
                                P.tr(ctp[pb][:n, i, :], srcT[:, t0:t0 + n], identb[:, :], [sk_, "identb"], ["ctp%d" % pb])
                            nfull = len([g for g in grp if g[2] == 128])
                            if cc < 8:
                                if nfull:
                                    P.copy(evac_eng(), Xtok[:, t4:t4 + nfull, cc * 128:(cc + 1) * 128], ctp[pb][:, 0:nfull, :], ["ctp%d" % pb], ["Xtok"])
                                if grp[-1][2] == 4:
                                    P.copy(evac_eng(), Xtok[:4, NT:NT + 1, cc * 128:(cc + 1) * 128], ctp[pb][:4, nfull:nfull + 1, :], ["ctp%d" % pb], ["Xtok"])
                            else:
                                if nfull:
                                    P.copy(evac_eng(), Btok[:, t4:t4 + nfull, cc - 8, :], ctp[pb][:, 0:nfull, :], ["ctp%d" % pb], ["Btok"])
                                if grp[-1][2] == 4:
                                    P.copy(evac_eng(), Btok[:4, NT:NT + 1, cc - 8, :], ctp[pb][:4, nfull:nfull + 1, :], ["ctp%d" % pb], ["Btok"])
                    P.flush()
                with ExitStack() as sd:
                    rhs_all = sb("rhs_all", [128, 16, 128], F32, sd); LT = sb("LT", [128, 16, 128], F32, sd)
                    STt = sb("STt", [128, 16, 128], BF16, sd); GTm = sb("GTm", [128, 2, 128], F32, sd)
                    xdt = sb("xdt", [128, 16, 64], BF16, sd); xdtw = sb("xdtw", [128, 16, 64], BF16, sd)
                    ea = sb("ea", [128, 16], F32, sd); dec = sb("dec", [128, 16], F32, sd)
                    yt = sb("yt", [128, 16, 64], F32, sd); y2 = sb("y2", [128, 16, 64], F32, sd); dx = yt
                    zt = sb("zt", [128, 1024], F32, sd); yn = sb("yn", [128, 1024], BF16, sd); yjunk = sb("yjunk", [128, 512], BF16, sd)
                    gs = sb("gs", [128, 8], F32, sd)
                    hT = sb("hT", [128, 16, 64], F32, sd); hT16 = sb("hT16", [128, 16, 64], BF16, sd)
                    hout = sb("hout", [64, 16, 128], F32, sd); h0 = hout
                    seg_ps = ps("seg_ps", [128, 8, 128], F32, sd); gt_ps = ps("gt_ps", [128, 512], F32, sd)
                    y_ps = ps("y_ps", [128, 16, 64], F32, sd); yo_ps = ps("yo_ps", [128, 16, 64], F32, sd)
                    yT_ps = y_ps[:, 0:8, :].bitcast(BF16)

                    def ssd_chunk(t, t0, n):
                        tok = slice(t0, t0 + n)
                        P.pump(1)
                        P.tt("pool", rhs_all[:n, :, :n], tri[:n, :n].unsqueeze(1).to_broadcast([n, 16, n]),
                             dtA[:n, t, :].unsqueeze(2).to_broadcast([n, 16, n]), ALU.mult, ["tri", "dtA"], ["rhs_all"])
                        for half in range(2):
                            for hh in range(8):
                                P.mm(seg_ps[:n, hh, :n], umat[:n, :n], rhs_all[:n, half * 8 + hh, :n], ["umat", "rhs_all"], ["seg_ps"])
                            P.act(LT[:n, half * 8:half * 8 + 8, :n], seg_ps[:n, :, :n], AF.Exp, ["seg_ps"], ["LT"])
                        for g in range(2):
                            P.mm(gt_ps[:n, g * 128:g * 128 + n], BT[:, g, tok], CT[:, g, tok], ["BT", "CT"], ["gt_ps"])
                        P.mm(gt_ps[:n, 256:272], tri[:n, :n], dtA[:n, t, :], ["tri", "dtA"], ["gt_ps"])
                        P.mm(gt_ps[:, 272:288], ones[:n, :], dtA[:n, t, :], ["ones", "dtA"], ["gt_ps"])
                        P.tt("dve", GTm[:n, :, :n], gt_ps[:n, 0:256].rearrange("p (g l) -> p g l", g=2)[:, :, :n],
                             tri[:n, :n].unsqueeze(1).to_broadcast([n, 2, n]), ALU.mult, ["gt_ps", "tri"], ["GTm"])
                        P.act(ea[:n, :], gt_ps[:n, 256:272], AF.Exp, ["gt_ps"], ["ea"])
                        P.act(dec[:, :], gt_ps[:, 272:288], AF.Exp, ["gt_ps"], ["dec"])
                        for g in range(2):
                            P.tt("dve", STt[:n, g * 8:g * 8 + 8, :n], LT[:n, g * 8:g * 8 + 8, :n],
                                 GTm[:n, g, :n].unsqueeze(1).to_broadcast([n, 8, n]), ALU.mult, ["LT", "GTm"], ["STt"])
                        xv = Xtok[:n, t, :].rearrange("p (h d) -> p h d", h=16)
                        P.tt("pool", xdt[:n], xv, dtv[:n, t, :].unsqueeze(2).to_broadcast([n, 16, 64]), ALU.mult, ["Xtok", "dtv"], ["xdt"])
                        P.tt("pool", xdtw[:n], xdt[:n], LT[:n, :, n - 1:n].to_broadcast([n, 16, 64]), ALU.mult, ["xdt", "LT"], ["xdtw"])
                        for h in range(16):
                            P.mm(y_ps[:n, h, :], STt[:n, h, :n], xdt[:n, h, :], ["STt", "xdt"], ["y_ps"])
                        for h in range(16):
                            P.mm(yo_ps[:n, h, :], CT[:, h // 8, tok], hT16[:, h, :], ["CT", "hT16"], ["yo_ps"])
                        P.tt("dve", yt[:n], yo_ps[:n], ea[:n, :].unsqueeze(2).to_broadcast([n, 16, 64]), ALU.mult, ["yo_ps", "ea"], ["yt"])
                        P.tt("dve", y2[:n], yt[:n], y_ps[:n], ALU.add, ["yt", "y_ps"], ["y2"])
                        P.tt("pool", dx[:n], xv, dsk[:n, :].unsqueeze(2).to_broadcast([n, 16, 64]), ALU.mult, ["Xtok", "dsk"], ["yt"])
                        P.tt("pool", y2[:n], y2[:n], dx[:n], ALU.add, ["y2", "yt"], ["y2"])
                        P.dma("sp", zt[:n], ZS[tok, :], ["ZS"], ["zt"])
                        y2f = y2[:n].rearrange("p h d -> p (h d)")
                        P.tt("dve", y2f, y2f, zt[:n], ALU.mult, ["y2", "zt"], ["y2"])
                        P.memset("pool", gs[:n, 0:2], 0.0, ["gs"])
                        for g in range(2):
                            P.act(yjunk[:n], y2f[:, g * 512:(g + 1) * 512], AF.Square, ["y2", "gs"], ["yjunk", "gs"], accum_out=gs[:n, g:g + 1])
                        P.act(gs[:n, 2:4], gs[:n, 0:2], AF.Sqrt, ["gs"], ["gs"], scale=1.0 / 512, bias=EPS)
                        P.recip(gs[:n, 4:6], gs[:n, 2:4], ["gs"], ["gs"])
                        for g in range(2):
                            P.stt("dve", yn[:n, g * 512:(g + 1) * 512], y2f[:, g * 512:(g + 1) * 512], gs[:n, 4 + g:5 + g],
                                  sng[:n, g * 512:(g + 1) * 512], ALU.mult, ALU.mult, ["y2", "gs", "sng"], ["yn"])
                        for c in range(8):
                            P.mm(seg_ps[:, c, :n], yn[:n, c * 128:(c + 1) * 128], identb[:n, :n], ["yn", "identb"], ["seg_ps"])
                        P.copy("act", YOT[:, 0:8, tok], seg_ps[:, :, :n], ["seg_ps"], ["YOT"])
                        for h in range(16):
                            P.mm(yo_ps[:, h, :], Btok[:n, t, h // 8, :], xdtw[:n, h, :], ["Btok", "xdtw"], ["yo_ps"])
                        P.tt("dve", hT[:], hT[:], dec[:, :].unsqueeze(2).to_broadcast([128, 16, 64]), ALU.mult, ["hT", "dec"], ["hT"])
                        P.tt("dve", hT[:], hT[:], yo_ps[:], ALU.add, ["hT", "yo_ps"], ["hT"])
                        P.copy("act", hT16[:], hT[:], ["hT"], ["hT16"])

                    def state_out(dst):
                        for half in range(2):
                            for hh in range(8):
                                P.tr(seg_ps[:64, hh, :], hT[:, half * 8 + hh, :], identf[:, :], ["hT", "identf"], ["seg_ps"])
                            P.copy("act", hout[:, half * 8:half * 8 + 8, :], seg_ps[:64, :, :], ["seg_ps"], ["hout"])
                        P.dma("sp", dst.rearrange("h p n -> p h n"), hout[:], ["hout"], (), is_out=True)

                    P.memset("dve", hT[:], 0.0, ["hT"]); P.memset("pool", hT16[:], 0.0, ["hT16"])
                    for (t, t0, n) in tiles[:NT]:
                        ssd_chunk(t, t0, n)
                    state_out(ssm_prompt[l])
                    P.dma("sp", h0[:], state_ssm[l].rearrange("h p n -> p h n"), (), ["hout"])
                    for h in range(16):
                        P.tr(yo_ps[:, h, :], h0[:, h, :], identf[:64, :64], ["hout", "identf"], ["yo_ps"])
                    P.copy("dve", hT[:], yo_ps[:], ["yo_ps"], ["hT"])
                    P.copy("act", hT16[:], hT[:], ["hT"], ["hT16"])
                    ssd_chunk(NT, SEQ, 4)
                    state_out(ssm_sample[l])
                    P.flush()

            with ExitStack() as ph:
                subg = load_bcast(ph, "subg", subln_g[l:l + 1, :], 128)
                P.ts("dve", subg[:], subg[:], 1.0 - lam_init, ALU.mult, ["subg"], ["subg"])
                QTh = [sb("QTh%d" % i, [128, SEQ], BF16, ph) for i in range(2)]
                KTh = [sb("KTh%d" % i, [128, SEQ], BF16, ph) for i in range(2)]
                vf = sb("vf", [128, NT, 128], F32, ph)
                Vaug = [sb("Vaug%d" % i, [128, NT, 129], BF16, ph) for i in range(2)]
                Ef = [sb("Ef%d" % i, [128, 4, 128], BF16, ph) for i in range(2)]
                En = [sb("En%d" % i, [128, 2, 128], BF16, ph) for i in range(2)]
                tmpn = sb("tmpn", [128, 2, 128], F32, ph)
                rz = sb("rz", [128, 8], F32, ph); o1 = sb("o1", [128, 128], F32, ph); o2 = sb("o2", [128, 128], F32, ph)
                on = sb("on", [128, 128], BF16, ph); ojunk = sb("ojunk", [128, 128], BF16, ph)
                sf_ps = [ps("sf_ps%d" % i, [128, 4, 128], F32, ph) for i in range(2)]
                sn_ps = ps("sn_ps", [128, 2, 128], F32, ph)
                o_ps = [ps("o_ps%d" % i, [128, 512], F32, ph) for i in range(2)]
                oT_ps = ps("oT_ps", [128, 128], BF16, ph)
                for i in range(2):
                    P.memset("pool", Vaug[i][:, :, 128:129], 1.0, ["Vaug%d" % i])
                fk = 0
                for h in range(8):
                    hb = h % 2
                    hs = slice(h * 128, (h + 1) * 128)
                    P.dma("sp", QTh[hb][:], QT[hs, 0:SEQ], ["QT"], ["QTh%d" % hb])
                    P.dma("sp", KTh[hb][:], KT[hs, 0:SEQ], ["KT"], ["KTh%d" % hb])
                    P.dma("sp", vf[:], v_prompt[l, :, hs].rearrange("(t p) e -> p t e", p=128), ["vout"], ["vf"])
                    P.copy("pool", Vaug[hb][:, :, 0:128], vf[:], ["vf"], ["Vaug%d" % hb])
                    for qb in range(NT):
                        qs = slice(qb * 128, (qb + 1) * 128)
                        P.pump(1)
                        for m in range(2):
                            ms = slice(m * 64, (m + 1) * 64)
                            far = list(range(0, qb - 1))
                            near = [qb - 1, qb] if qb >= 1 else [qb]
                            nblk = qb + 1
                            done = 0
                            for g0 in range(0, len(far), 4):
                                grp = far[g0:g0 + 4]
                                fb = fk % 2; fk += 1
                                for i, kb in enumerate(grp):
                                    P.mm(sf_ps[fb][:, i, :], KTh[hb][ms, kb * 128:(kb + 1) * 128], QTh[hb][ms, qs], ["KTh%d" % hb, "QTh%d" % hb], ["sf_ps%d" % fb])
                                P.act(Ef[fb][:, :len(grp), :], sf_ps[fb][:, :len(grp), :], AF.Exp, ["sf_ps%d" % fb], ["Ef%d" % fb], scale=0.125)
                                for i, kb in enumerate(grp):
                                    P.mm(o_ps[m][:, 0:129], Ef[fb][:, i, :], Vaug[hb][:, kb, :], ["Ef%d" % fb, "Vaug%d" % hb], ["o_ps%d" % m], start=(done == 0), stop=False)
                                    done += 1
                            kn = len(near)
                            for i, kb in enumerate(near):
                                P.mm(sn_ps[:, i, :], KTh[hb][ms, kb * 128:(kb + 1) * 128], QTh[hb][ms, qs], ["KTh%d" % hb, "QTh%d" % hb], ["sn_ps"])
                            P.stt("dve", tmpn[:, :kn, :], sn_ps[:, :kn, :], 0.125, Tb[:, h, (2 - kn) * 128:256].rearrange("p (b q) -> p b q", q=128),
                                  ALU.mult, ALU.add, ["sn_ps", "Tb"], ["tmpn"])
                            P.act(En[m][:, :kn, :], tmpn[:, :kn, :], AF.Exp, ["tmpn"], ["En%d" % m])
                            for i, kb in enumerate(near):
                                P.mm(o_ps[m][:, 0:129], En[m][:, i, :], Vaug[hb][:, kb, :], ["En%d" % m, "Vaug%d" % hb], ["o_ps%d" % m], start=(done == 0), stop=(i == kn - 1))
                                done += 1
                        P.recip(rz[:, 0:1], o_ps[0][:, 128:129], ["o_ps0"], ["rz"])
                        P.recip(rz[:, 1:2], o_ps[1][:, 128:129], ["o_ps1"], ["rz"])
                        P.tt("dve", rz[:, 2:3], rz[:, 1:2], nlam[:, l:l + 1], ALU.mult, ["rz", "nlam"], ["rz"])
                        P.ts("dve", o1[:], o_ps[0][:, 0:128], rz[:, 0:1], ALU.mult, ["o_ps0", "rz"], ["o1"])
                        P.stt("dve", o2[:], o_ps[1][:, 0:128], rz[:, 2:3], o1[:], ALU.mult, ALU.add, ["o_ps1", "rz", "o1"], ["o2"])
                        P.memset("pool", rz[:, 3:4], 0.0, ["rz3"])
                        P.act(ojunk[:], o2[:], AF.Square, ["o2", "rz3"], ["ojunk", "rz3"], accum_out=rz[:, 3:4])
                        P.act(rz[:, 4:5], rz[:, 3:4], AF.Sqrt, ["rz3"], ["rz4"], scale=1.0 / 128, bias=EPS)
                        P.recip(rz[:, 5:6], rz[:, 4:5], ["rz4"], ["rz5"])
                        P.stt("dve", on[:], o2[:], rz[:, 5:6], subg[:], ALU.mult, ALU.mult, ["o2", "rz5", "subg"], ["on"])
                        P.tr(oT_ps[:], on[:], identb[:], ["on", "identb"], ["oT_ps"])
                        P.copy("act", YOT[:, 8 + h, qs], oT_ps[:], ["oT_ps"], ["YOT"])
                P.flush()

            with ExitStack() as ph:
                subg = load_bcast(ph, "subg2", subln_g[l:l + 1, :], 128)
                P.ts("dve", subg[:], subg[:], 1.0 - lam_init, ALU.mult, ["subg2"], ["subg2"])
                QTs = sb("QTs", [128, 8, 4], BF16, ph); KTn = sb("KTn", [128, 8, 128], BF16, ph)
                Kpg = [sb("Kpg%d" % i, [128, 1024], F32, ph) for i in range(3)]
                Vpg = [sb("Vpg%d" % i, [128, 1024], F32, ph) for i in range(3)]
                K16 = [sb("K16_%d" % i, [128, 1024], BF16, ph) for i in range(2)]
                KTp = [sb("KTp%d" % i, [128, 8, 128], BF16, ph) for i in range(2)]
                Vg = [sb("Vg%d" % i, [128, 8, 129], BF16, ph) for i in range(2)]
                Es = [sb("Es%d" % i, [128, 8, 2, 4], BF16, ph) for i in range(2)]
                tms = sb("tms", [128, 8, 2, 4], F32, ph)
                acc = sb("sacc", [8, 3, 512], F32, ph); osc = sb("osc", [8, 8, 128], F32, ph); rzs = sb("rzs", [8, 16], F32, ph)
                scv = sb("scv", [8, 1], F32, ph)
                of = sb("of", [4, 8, 128], F32, ph); osq = sb("osq", [4, 8, 128], F32, ph); oss = sb("oss", [4, 24], F32, ph)
                onb = sb("onb", [4, 8, 128], BF16, ph)
                kt_ps = [ps("kt_ps%d" % i, [128, 8, 128], BF16, ph) for i in range(2)]
                s_ps = [ps("s_ps%d" % i, [128, 8, 2, 4], F32, ph) for i in range(2)]
                os_ps = ps("os_ps", [8, 3, 512], F32, ph)
                P.dma("sp", QTs[:], QT[:, SEQ:SEQ + 4].rearrange("(h p) q -> p h q", p=128), ["QT"], ["QTs"])
                P.memset("dve", KTn[:], 0.0, ["KTn"])
                P.dma("sp", KTn[:, :, 0:4], KT[:, SEQ:SEQ + 4].rearrange("(h p) q -> p h q", p=128), ["KT"], ["KTn"])
                P.memset("dve", acc[:], 0.0, ["sacc"])
                for i in range(2):
                    P.memset("pool", Vg[i][:, :, 128:129], 1.0, ["Vg%d" % i])

                def accv(a, bk_, nr):
                    return a[:, bk_, 0:480].rearrange("p (r c) -> p r c", c=160)[:, 0:nr, 0:129]

                def score_av(j, nk, KTsrc, ktk, vsrc_fn, bias_ap):
                    b2 = j % 2
                    for h in range(8):
                        for m in range(2):
                            P.mm(s_ps[b2][:nk, h, m, :], KTsrc[m * 64:(m + 1) * 64, h, :nk], QTs[m * 64:(m + 1) * 64, h, :], [ktk, "QTs"], ["s_ps%d" % b2])
                    if bias_ap is None:
                        P.act(Es[b2][:nk], s_ps[b2][:nk], AF.Exp, ["s_ps%d" % b2], ["Es%d" % b2], scale=0.125)
                    else:
                        for m in range(2):
                            P.stt("dve", tms[:nk, :, m, :], s_ps[b2][:nk, :, m, :], 0.125, bias_ap, ALU.mult, ALU.add, ["s_ps%d" % b2, "Tb"], ["tms"])
                        P.act(Es[b2][:nk], tms[:nk], AF.Exp, ["tms"], ["Es%d" % b2])
                    vsrc_fn(b2)
                    for h in range(8):
                        P.mm(os_ps[:, h // 3, (h % 3) * 160:(h % 3) * 160 + 129], Es[b2][:, h, :, :].rearrange("p m q -> p (m q)"), Vg[b2][:, h, :],
                             ["Es%d" % b2, "Vg%d" % b2], ["os_ps"])
                    for bk_ in range(3):
                        nr = 3 if bk_ < 2 else 2
                        P.tt("dve", accv(acc, bk_, nr), accv(acc, bk_, nr), accv(os_ps, bk_, nr), ALU.add, ["sacc", "os_ps"], ["sacc"])

                for j in range(NPAGES):
                    b3 = j % 3; b2 = j % 2
                    P.gather(Kpg[b3][:], cache_k, idx_all[:, l, j:j + 1], ["idx"], ["Kpg%d" % b3])
                    P.gather(Vpg[b3][:], cache_v, idx_all[:, l, j:j + 1], ["idx"], ["Vpg%d" % b3])
                    P.copy("act", K16[b2][:], Kpg[b3][:], ["Kpg%d" % b3], ["K16_%d" % b2])
                    for h in range(8):
                        P.tr(kt_ps[b2][:, h, :], K16[b2][:, h * 128:(h + 1) * 128], identb[:], ["K16_%d" % b2, "identb"], ["kt_ps%d" % b2])
                    P.copy("dve", KTp[b2][:], kt_ps[b2][:], ["kt_ps%d" % b2], ["KTp%d" % b2])

                    def vfn(bb, b3=b3):
                        P.copy("pool", Vg[bb][:, :, 0:128], Vpg[b3][:].rearrange("p (h e) -> p h e", h=8), ["Vpg%d" % b3], ["Vg%d" % bb])
                    bias_ap = None
                    if j == NPAGES - 1:
                        bias_ap = Tb[:, :, 256:260]
                    score_av(j, 128, KTp[b2], "KTp%d" % b2, vfn, bias_ap)
                vnew = sb("vnew", [4, 1024], F32, ph)
                P.dma("sp", vnew[:], v_sample[l], ["vout"], ["vnew"])

                def vfn2(bb):
                    P.copy("pool", Vg[bb][:4, :, 0:128], vnew[:].rearrange("p (h e) -> p h e", h=8), ["vnew"], ["Vg%d" % bb])
                score_av(NPAGES, 128, KTn, "KTn", vfn2, Tb[:, :, 260:264])
                for h in range(8):
                    P.recip(rzs[:, h:h + 1], acc[:, h // 3, (h % 3) * 160 + 128:(h % 3) * 160 + 129], ["sacc"], ["rzs"])
                P.stt("dve", scv[:], cmm[:, 1:2], nlam[:8, l:l + 1], cmm[:, 0:1], ALU.mult, ALU.add, ["cmm", "nlam"], ["scv"])
                P.ts("dve", rzs[:, 8:16], rzs[:, 0:8], scv[:, 0:1], ALU.mult, ["rzs", "scv"], ["rzs"])
                for h in range(8):
                    P.ts("dve", osc[:, h, :], acc[:, h // 3, (h % 3) * 160:(h % 3) * 160 + 128], rzs[:, 8 + h:9 + h], ALU.mult, ["sacc", "rzs"], ["osc"])
                for c in range(2):
                    P.mm(os_ps[:4, c, :], selm[:, :], osc[:, c * 4:(c + 1) * 4, :].rearrange("p h e -> p (h e)"), ["selm", "osc"], ["os_ps"])
                P.copy("act", of[:].rearrange("p (c h) e -> p c (h e)", c=2), os_ps[:4, 0:2, :], ["os_ps"], ["of"])
                P.tt("dve", osq[:], of[:], of[:], ALU.mult, ["of"], ["osq"])
                P.rsum(oss[:, 0:8], osq[:], ["osq"], ["oss"])
                P.act(oss[:, 8:16], oss[:, 0:8], AF.Sqrt, ["oss"], ["oss"], scale=1.0 / 128, bias=EPS)
                P.recip(oss[:, 16:24], oss[:, 8:16], ["oss"], ["oss"])
                P.tt("dve", osq[:], of[:], oss[:, 16:24].unsqueeze(2).to_broadcast([4, 8, 128]), ALU.mult, ["of", "oss"], ["osq"])
                P.tt("dve", onb[:], osq[:], subg[:4, :].unsqueeze(1).to_broadcast([4, 8, 128]), ALU.mult, ["osq", "subg2"], ["onb"])
                for h in range(8):
                    P.tr(kt_ps[0][:, h, 0:4], onb[:, h, :], identb[:4, :4], ["onb", "identb"], ["kt_ps0"])
                P.copy("act", YOT[:, 8:16, SEQ:SEQ + 4], kt_ps[0][:, :, 0:4], ["kt_ps0"], ["YOT"])
                P.flush()

            with ExitStack() as ph:
                s32 = Stage(ph, "r32_", [128, 512], F32, 4)
                xpc = Stage(ph, "xpc_", [128, 512], F32, 4)
                srcf = x_src(l)

                def sink_res(t, t0, n, c0, w, pap, pk):
                    xb, xk = xpc.next()
                    st, sk = s32.next()
                    xkey = "Xr_%d_%d" % (t, c0)
                    P.dma("sp", xb[:n, :w], srcf(t, t0, n)[:, c0:c0 + w], [xkey], [xk])
                    P.tt("dve", st[:n, :w], pap, xb[:n, :w], ALU.add, [pk, xk], [sk])
                    P.dma("sp", X[t0:t0 + n, c0:c0 + w], st[:n, :w], [sk], [xkey])
                phase_proj(w_out[l], YOT, [(c0, 512, "T", sink_res) for c0 in range(0, D, 512)], tag="po")

        phase_norm(lambda t, t0, n: X[t0:t0 + n, :], norm_ffn_g[l:l + 1, :], dstT_dram=N2T, tag="nf")

        with ExitStack() as ph:
            n2T = sb("n2T", [128, DC, NTOK], BF16, ph)
            qT = sb("qT", [128, 16, NTOK], BF16, ph)
            P.dma("sp", n2T[:], N2T.rearrange("(c p) t -> p c t", p=128), ["N2T"], ["nT"])

            def sink_q(cs, g0, gn, pap, pk):
                P.copy(evac_eng(), qT[:, cs // 128, g0:g0 + gn], pap, [pk], ["qT"])
            phase_proj(peer_wq[l], n2T, [(c0, 512, "F", sink_q) for c0 in range(0, D, 512)], tag="pq")
            with ExitStack() as p6:
                skf = sb("skf", [128, 8, 2, NK], F32, p6); skT = sb("skT", [128, 8, 2, NK], BF16, p6)
                P.dma("sp", skf[:, :, 0, :], sk1T[l].rearrange("h d k -> d h k"), (), ["skf"])
                P.dma("sp", skf[:, :, 1, :], sk2T[l].rearrange("h d k -> d h k"), (), ["skf"])
                P.copy("dve", skT[:], skf[:], ["skf"], ["skT"])
                Ssb = [sb("Ssb%d" % i, [128, 8, 2, NK], F32, p6) for i in range(2)]
                wk = sb("wk", [128, NK], F32, p6)
                v12 = sb("v12", [128, 2, 16], F32, p6)
                cand = sb("cand", [128, 256], F32, p6); cwk = sb("cwk", [128, 256], F32, p6); cex = sb("cex", [128, 256], F32, p6)
                cv = sb("cv", [128, 16], F32, p6); nm = sb("nm", [128, 8], F32, p6); Zt = sb("Zt", [128, 8], F32, p6)
                sc_ps = ps("sc_ps", [128, 8, 2, NK], F32, p6)
                for (t, t0, n) in tiles:
                    b = t % 2
                    for h in range(8):
                        for a in range(2):
                            P.mm(sc_ps[:n, h, a, :], qT[:, 2 * h + a, t0:t0 + n], skT[:, h, a, :], ["qT", "skT"], ["sc_ps"])
                    P.copy("act", Ssb[b][:n], sc_ps[:n], ["sc_ps"], ["Ssb%d" % b])
                    P.dma("sp", SC[t0:t0 + n, :], Ssb[b][:n].rearrange("p h a k -> p (h a k)"), ["Ssb%d" % b], ())
                    for h in range(8):
                        for a in range(2):
                            P.vmax(v12[:n, a, 0:8], Ssb[b][:n, h, a, :], ["Ssb%d" % b], ["v12"])
                            P.mrep(wk[:n], v12[:n, a, 0:8], Ssb[b][:n, h, a, :], ["Ssb%d" % b, "v12"], ["wk"])
                            P.vmax(v12[:n, a, 8:16], wk[:n], ["wk"], ["v12"])
                        c3 = cand[:n].rearrange("p (a b) -> p a b", b=16)
                        P.tt("dve", c3, v12[:n, 0, :].unsqueeze(2).to_broadcast([n, 16, 16]), v12[:n, 1, :].unsqueeze(1).to_broadcast([n, 16, 16]), ALU.add, ["v12"], ["cand"])
                        P.vmax(cv[:n, 0:8], cand[:n], ["cand"], ["cv"])
                        P.mrep(cwk[:n], cv[:n, 0:8], cand[:n], ["cand", "cv"], ["cwk"])
                        P.vmax(cv[:n, 8:16], cwk[:n], ["cwk"], ["cv"])
                        P.copy("pool", TAU[:n, t, h:h + 1], cv[:n, 15:16], ["cv"], ["TAU"])
                        P.ts("dve", nm[:n, h:h + 1], cv[:n, 0:1], -1.0, ALU.mult, ["cv"], ["nm"])
                        P.act(cex[:n], cand[:n], AF.Exp, ["cand", "nm"], ["cex"], bias=nm[:n, h:h + 1])
                        P.stt("dve", cwk[:n], cand[:n], cv[:n, 15:16], cex[:n], ALU.is_ge, ALU.mult, ["cand", "cv", "cex"], ["cwk"])
                        P.rsum(Zt[:n, h:h + 1], cwk[:n], ["cwk"], ["Zt"])
                    P.act(Zt[:n], Zt[:n], AF.Ln, ["Zt"], ["Zt"])
                    P.tt("dve", NB[:n, t, :], nm[:n], Zt[:n], ALU.subtract, ["nm", "Zt"], ["NB"])
                P.flush()

        P.pump(10000)
        with ExitStack() as ph:
            ukeys = P.bgkeys.get("UT16_%d" % l, [])
            vkeys = P.bgkeys.get("V16_%d" % l, [])
            PIECE = max(512, NE // 16)
            CPP = PIECE // 512
            IPP = PIECE // NK
            pgroups = []
            tl = list(tiles)
            tp_ = tl[:NT]
            for g0 in range(0, NT, 4):
                pgroups.append(tp_[g0:g0 + 4])
            pgroups[-1] = pgroups[-1] + [tl[NT]]
            MAXG = max(len(g) for g in pgroups)
            MAXTOK = max(sum(n for (_, _, n) in g) for g in pgroups)
            n2g = sb("n2g", [128, DC, MAXTOK], BF16, ph)
            SCt = [sb("SCt%d" % i, [128, 8, 2, NK], F32, ph) for i in range(2)]
            T1 = sb("T1", [128, IPP, NK], F32, ph); Wx = sb("Wx", [128, IPP, NK], F32, ph); Gh = sb("Gh", [128, PIECE], BF16, ph)
            G = sb("G", [128, MAXG, PIECE], BF16, ph)
            Uc = [sb("Uc%d" % i, [128, DC, 512], BF16, ph) for i in range(2)]
            Vc = [sb("Vc%d" % i, [128, 4, D], BF16, ph) for i in range(2)]
            pacc = sb("pacc", [128, MAXG, D], F32, ph)
            Hg = [sb("Hg%d" % i, [128, 512], F32, ph) for i in range(2)]
            Ab = [sb("Ab%d" % i, [128, 512], BF16, ph) for i in range(2)]
            ATs = [sb("ATs%d" % i, [128, 4, 128], BF16, ph) for i in range(2)]
            xres = [sb("xres0", [128, D], F32, ph)]
            h_ps = [ps("h_ps%d" % i, [128, 512], F32, ph) for i in range(2)]
            at_ps = ps("at_ps", [128, 4, 128], BF16, ph)
            op_ps = ps("op_ps", [128, 4, 512], F32, ph)
            ck = 0
            sck = [0]
            for gi, grp in enumerate(pgroups):
                gt0 = grp[0][1]
                gtok = sum(n for (_, _, n) in grp)
                P.dma("sp", n2g[:, :, :gtok], N2T[:, gt0:gt0 + gtok].rearrange("(c p) t -> p c t", p=128), ["N2T"], ["n2g"])
                for pc in range(NE // PIECE):
                    for ti, (t, t0, n) in enumerate(grp):
                        sck[0] += 1; sb_ = sck[0] % 2
                        P.dma("sp", SCt[sb_][:n].rearrange("p h a k -> p (h a k)"), SC[t0:t0 + n, :], ["SC"], ["SCt%d" % sb_])
                        for h in range(8):
                            P.tt("pool", T1[:n], SCt[sb_][:n, h, 0, pc * IPP:(pc + 1) * IPP].unsqueeze(2).to_broadcast([n, IPP, NK]),
                                 SCt[sb_][:n, h, 1, :].unsqueeze(1).to_broadcast([n, IPP, NK]), ALU.add, ["SCt%d" % sb_], ["T1"])
                            P.act(Wx[:n], T1[:n], AF.Exp, ["T1", "NB"], ["Wx"], bias=NB[:n, t, h:h + 1])
                            T1f = T1[:n].rearrange("p i k -> p (i k)"); Wf = Wx[:n].rearrange("p i k -> p (i k)")
                            if h == 0:
                                P.stt("dve", G[:n, ti, :], T1f, TAU[:n, t, h:h + 1], Wf, ALU.is_ge, ALU.mult, ["T1", "Wx", "TAU"], ["G%d" % ti])
                            else:
                                P.stt("dve", Gh[:n], T1f, TAU[:n, t, h:h + 1], Wf, ALU.is_ge, ALU.mult, ["T1", "Wx", "TAU"], ["Gh"])
                                P.tt("dve", G[:n, ti, :], G[:n, ti, :], Gh[:n], ALU.add, ["G%d" % ti, "Gh"], ["G%d" % ti])
                    for c in range(CPP):
                        e0 = pc * PIECE + c * 512
                        ub = ck % 2; ck += 1
                        P.dma("sp", Uc[ub][:], UT16[l, :, e0:e0 + 512].rearrange("(c p) e -> p c e", p=128), ukeys, ["Uc%d" % ub])
                        P.dma("sp", Vc[ub][:], V16[l, e0:e0 + 512, :].rearrange("(s p) d -> p s d", p=128), vkeys, ["Vc%d" % ub])
                        first = (pc == 0 and c == 0)
                        toff = 0
                        for ti, (t, t0, n) in enumerate(grp):
                            hb = (ck + ti) % 2
                            for dc in range(DC):
                                P.mm(h_ps[hb][:n, :], n2g[:, dc, toff:toff + n], Uc[ub][:, dc, :], ["n2g", "Uc%d" % ub], ["h_ps%d" % hb], start=(dc == 0), stop=(dc == DC - 1))
                            P.act(Hg[hb][:n], h_ps[hb][:n, :], AF.Gelu, ["h_ps%d" % hb], ["Hg%d" % hb])
                            P.tt("dve", Ab[hb][:n], G[:n, ti, c * 512:(c + 1) * 512], Hg[hb][:n], ALU.mult, ["G%d" % ti, "Hg%d" % hb], ["Ab%d" % hb])
                            for s in range(4):
                                P.tr(at_ps[:, s, :n], Ab[hb][:n, s * 128:(s + 1) * 128], identb[:n, :n], ["Ab%d" % hb, "identb"], ["at_ps"])
                            P.copy("act", ATs[hb][:, :, :n], at_ps[:, :, :n], ["at_ps"], ["ATs%d" % hb])
                            for dq in range(4):
                                for s in range(4):
                                    P.mm(op_ps[:n, dq, :], ATs[hb][:, s, :n], Vc[ub][:, s, dq * 512:(dq + 1) * 512], ["ATs%d" % hb, "Vc%d" % ub], ["op_ps"], start=(s == 0), stop=(s == 3))
                            pa = pacc[:n, ti, :].rearrange("p (q d) -> p q d", q=4)
                            if first:
                                P.copy("dve", pa, op_ps[:n], ["op_ps"], ["pacc%d" % ti])
                            else:
                                P.tt("dve", pa, pa, op_ps[:n], ALU.add, ["op_ps", "pacc%d" % ti], ["pacc%d" % ti])
                            toff += n
                for ti, (t, t0, n) in enumerate(grp):
                    xb = 0
                    P.dma("sp", xres[xb][:n], X[t0:t0 + n, :], (), ["xres%d" % xb])
                    P.tt("pool", xres[xb][:n], xres[xb][:n], pacc[:n, ti, :], ALU.add, ["xres%d" % xb, "pacc%d" % ti], ["xres%d" % xb])
                    P.dma("sp", X[t0:t0 + n, :], xres[xb][:n], ["xres%d" % xb], ())
            P.flush()

    phase_norm(lambda t, t0, n: X[t0:t0 + n, :], norm_final_g[0:1, :],
               out_of_tile=lambda t, t0, n: (y_prompt[t0:t0 + n, :] if t0 < SEQ else y_sample[0:4, :]), tag="nz")
    P.flush(final=True)
    es.close()
    return nc, P


def make_in_maps(cfg, inp, n_cores):
    SEQ, NPAGES, NPOOL, NK, DEPTH = cfg["SEQ"], cfg["NPAGES"], cfg["NPOOL"], cfg["NK"], cfg["DEPTH"]
    f = lambda a: np.ascontiguousarray(np.asarray(a, dtype=np.float32))
    bk, mk, sel, cm = host_consts()
    BATCH = inp["x_prompt"].shape[0]
    shared = {
        "cache_k": f(inp["cache_k"]).reshape(DEPTH * NPOOL * 128, 1024),
        "cache_v": f(inp["cache_v"]).reshape(DEPTH * NPOOL * 128, 1024),
        "rel_bias": f(inp["rel_bias"]).reshape(1, 256),
        "norm_attn_g": f(inp["norm_attn_g"]), "w_in": f(inp["w_in"]),
        "conv_w": f(inp["conv_w"]), "conv_b": f(inp["conv_b"]),
        "a_log": f(inp["a_log"]), "dt_bias": f(inp["dt_bias"]), "d_skip": f(inp["d_skip"]),
        "ssm_norm_g": f(inp["ssm_norm_g"]), "lam_qk": f(inp["lam_qk"]).reshape(1, DEPTH * 256),
        "subln_g": f(inp["subln_g"]), "w_out": f(inp["w_out"]), "norm_ffn_g": f(inp["norm_ffn_g"]),
        "peer_wq": f(inp["peer_wq"]),
        "sk1T": f(np.transpose(np.asarray(inp["peer_sk1"]), (0, 1, 3, 2))),
        "sk2T": f(np.transpose(np.asarray(inp["peer_sk2"]), (0, 1, 3, 2))),
        "peer_uT": f(np.transpose(np.asarray(inp["peer_u"]), (0, 2, 1))),
        "peer_v": f(inp["peer_v"]),
        "norm_final_g": f(inp["norm_final_g"]).reshape(1, D),
        "c_bk": bk, "c_mk": mk, "c_sel": sel, "c_cm": cm,
    }
    maps = []
    for c in range(n_cores):
        m = dict(shared)
        m["xp"] = f(inp["x_prompt"][c % BATCH])
        m["xs"] = f(inp["x_sample"][c])
        m["state_ssm"] = f(np.asarray(inp["state_ssm"])[:, c])
        m["state_conv"] = f(np.asarray(inp["state_conv"])[:, c])
        m["page_table"] = np.ascontiguousarray(np.asarray(inp["page_table"], dtype=np.int32)[c:c + 1, :])
        maps.append(m)
    return maps


def assemble(cfg, res, n_cores, BATCH):
    SEQ, DEPTH = cfg["SEQ"], cfg["DEPTH"]
    pc = list(range(min(BATCH, n_cores)))
    sc = list(range(n_cores))
    y_prompt = np.stack([res[c]["y_prompt"] for c in pc])
    y_sample = np.stack([res[c]["y_sample"] for c in sc])
    k_prompt = np.stack([res[c]["k_prompt"].reshape(DEPTH, SEQ, 8, 128) for c in pc], axis=1)
    v_prompt = np.stack([res[c]["v_prompt"].reshape(DEPTH, SEQ, 8, 128) for c in pc], axis=1)
    ssm_prompt = np.stack([res[c]["ssm_prompt"] for c in pc], axis=1)
    conv_prompt = np.stack([res[c]["conv_prompt"] for c in pc], axis=1)
    k_sample = np.stack([res[c]["k_sample"].reshape(DEPTH, 4, 8, 128) for c in sc], axis=1)
    v_sample = np.stack([res[c]["v_sample"].reshape(DEPTH, 4, 8, 128) for c in sc], axis=1)
    ssm_sample = np.stack([res[c]["ssm_sample"] for c in sc], axis=1)
    conv_sample = np.stack([res[c]["conv_sample"] for c in sc], axis=1)
    return tuple(np.ascontiguousarray(a.astype(np.float32)) for a in
                 (y_prompt, y_sample, k_prompt, v_prompt, ssm_prompt, conv_prompt, k_sample, v_sample, ssm_sample, conv_sample))


def kernel(**inputs):
    cfg = FULL
    nc, _ = build(cfg)
    maps = make_in_maps(cfg, inputs, 8)
    res = run_bass_kernel_spmd(nc, maps, core_ids=list(range(8)))
    return assemble(cfg, res.results, 8, 4)
```
